# Optimizing a Trainium2 kernel written in Bass

```python
import math
import jax, jax.numpy as jnp
from jax import lax
import numpy as np


D_MODEL = 2048
BATCH = 1
SEQ = 8192
DEPTH = 2

CHUNK = 64
Q_BLOCK = 128
N_MIXERS = 2
N_ATTN_LAYERS = (DEPTH + 1) // 2
N_GDN_LAYERS = DEPTH // 2
RMS_EPS = 1e-6

DIFF_HEADS = D_MODEL // 256
DIFF_HEAD_DIM = 128
DIFF_QK_DIM = 2 * DIFF_HEADS * DIFF_HEAD_DIM
DIFF_V_DIM = DIFF_HEADS * 2 * DIFF_HEAD_DIM
LAMBDA_STD = 0.1

GDN_HEAD_DIM = 128
GDN_QK_HEADS = D_MODEL // 128
GDN_V_HEADS = 2 * GDN_QK_HEADS
GDN_REPEAT = GDN_V_HEADS // GDN_QK_HEADS
GDN_KEY_DIM = GDN_QK_HEADS * GDN_HEAD_DIM
GDN_VALUE_DIM = GDN_V_HEADS * GDN_HEAD_DIM
GDN_CONV = 4
GDN_IN_DIM = 2 * GDN_KEY_DIM + 2 * GDN_VALUE_DIM + 2 * GDN_V_HEADS

D_FF = ((8 * D_MODEL // 3 + 127) // 128) * 128
FFN_CONV = 3

kernel_name = 'hybrid_diffattn_gdn_convffn'


def rms_norm(x, g):
    xf = x.astype(jnp.float32)
    y = xf * lax.rsqrt(jnp.mean(xf * xf, axis=-1, keepdims=True) + RMS_EPS)
    return (y * g.astype(jnp.float32)).astype(x.dtype)


def l2_norm(x):
    return x * lax.rsqrt(jnp.sum(x * x, axis=-1, keepdims=True) + RMS_EPS)


def causal_dwconv(u, w):
    k_width = w.shape[0]
    s = u.shape[1]
    up = jnp.pad(u, ((0, 0), (k_width - 1, 0), (0, 0)))
    out = up[:, 0:s] * w[0]
    for j in range(1, k_width):
        out = out + up[:, j:j + s] * w[j]
    return out


def alibi_slopes(n_heads):
    return jnp.exp2(-8.0 * jnp.arange(1, n_heads + 1, dtype=jnp.float32) / n_heads)


def diff_attention(h, w_qkv, q_norm, k_norm, lq1, lk1, lq2, lk2, subln, w_o, lambda_init):
    b_sz, s_len, _ = h.shape
    n_blocks = s_len // Q_BLOCK
    qkv = h @ w_qkv
    q, k, v = jnp.split(qkv, [DIFF_QK_DIM, 2 * DIFF_QK_DIM], axis=-1)
    q = rms_norm(q.reshape(b_sz, s_len, 2 * DIFF_HEADS, DIFF_HEAD_DIM), q_norm) * (DIFF_HEAD_DIM ** -0.5)
    k = rms_norm(k.reshape(b_sz, s_len, 2 * DIFF_HEADS, DIFF_HEAD_DIM), k_norm)
    v = v.reshape(b_sz, s_len, DIFF_HEADS, 2 * DIFF_HEAD_DIM)
    lam = (jnp.exp(jnp.sum(lq1.astype(jnp.float32) * lk1.astype(jnp.float32)))
           - jnp.exp(jnp.sum(lq2.astype(jnp.float32) * lk2.astype(jnp.float32))) + lambda_init)
    q_blocks = q.reshape(b_sz, n_blocks, Q_BLOCK, 2 * DIFF_HEADS, DIFF_HEAD_DIM).transpose(1, 0, 3, 2, 4)
    k_t = k.transpose(0, 2, 1, 3)
    v_t = v.transpose(0, 2, 1, 3)
    k_pos = jnp.arange(s_len, dtype=jnp.int32)
    k_chunk = k_pos // CHUNK
    slopes = jnp.repeat(alibi_slopes(DIFF_HEADS), 2)

    def block(args):
        q_blk, blk = args
        q_pos = blk * Q_BLOCK + jnp.arange(Q_BLOCK, dtype=jnp.int32)
        s = jnp.einsum('bhqd,bhkd->bhqk', q_blk, k_t).astype(jnp.float32)
        dist = jnp.abs(q_pos[:, None] - k_pos[None, :]).astype(jnp.float32)
        s = s - slopes[:, None, None] * dist
        allowed = k_chunk[None, :] <= (q_pos // CHUNK)[:, None]
        s = jnp.where(allowed, s, -jnp.inf)
        p = jax.nn.softmax(s, axis=-1).reshape(b_sz, DIFF_HEADS, 2, Q_BLOCK, s_len)
        a = p[:, :, 0] - lam * p[:, :, 1]
        return jnp.einsum('bhqk,bhke->bhqe', a.astype(v_t.dtype), v_t)

    o = lax.map(block, (q_blocks, jnp.arange(n_blocks, dtype=jnp.int32)))
    o = o.transpose(1, 0, 3, 2, 4).reshape(b_sz, s_len, DIFF_HEADS, 2 * DIFF_HEAD_DIM)
    o = rms_norm(o, subln) * (1.0 - lambda_init)
    return o.reshape(b_sz, s_len, DIFF_V_DIM) @ w_o


def gated_delta_rule(q, k, v, g, beta):
    b_sz, s_len, n_h, dk = q.shape
    dv = v.shape[-1]
    n_c = s_len // CHUNK
    q = q * (dk ** -0.5)

    def to_chunks(t):
        return t.reshape(b_sz, n_c, CHUNK, n_h, -1).transpose(0, 3, 1, 2, 4)

    q, k, v = to_chunks(q), to_chunks(k), to_chunks(v)
    beta_c = beta.reshape(b_sz, n_c, CHUNK, n_h).transpose(0, 3, 1, 2)
    g_c = jnp.cumsum(g.reshape(b_sz, n_c, CHUNK, n_h).transpose(0, 3, 1, 2), axis=-1)
    k_beta = k * beta_c[..., None]
    v_beta = v * beta_c[..., None]
    tril = jnp.tril(jnp.ones((CHUNK, CHUNK), dtype=bool))
    strict = jnp.tril(jnp.ones((CHUNK, CHUNK), dtype=bool), -1)
    decay = jnp.exp(jnp.where(tril, g_c[..., :, None] - g_c[..., None, :], -jnp.inf))
    low = jnp.where(strict, jnp.einsum('bhncd,bhnkd->bhnck', k_beta, k) * decay, 0.0)
    a_mat = low + jnp.eye(CHUNK, dtype=low.dtype)
    u = lax.linalg.triangular_solve(a_mat, v_beta, left_side=True, lower=True, unit_diagonal=True)
    w = lax.linalg.triangular_solve(a_mat, k_beta * jnp.exp(g_c)[..., None], left_side=True, lower=True, unit_diagonal=True)
    qk = jnp.where(tril, jnp.einsum('bhncd,bhnkd->bhnck', q, k) * decay, 0.0)
    q_dec = q * jnp.exp(g_c)[..., None]
    k_dec = k * jnp.exp(g_c[..., -1:] - g_c)[..., None]
    g_last = jnp.exp(g_c[..., -1])

    def step(state, xs):
        qk_n, u_n, w_n, qd_n, kd_n, gl_n = xs
        v_new = u_n - jnp.einsum('bhcd,bhde->bhce', w_n, state)
        o = jnp.einsum('bhcd,bhde->bhce', qd_n, state) + jnp.einsum('bhck,bhke->bhce', qk_n, v_new)
        state = state * gl_n[..., None, None] + jnp.einsum('bhcd,bhce->bhde', kd_n, v_new)
        return state, o

    xs = tuple(t.transpose(2, 0, 1, 3, 4) for t in (qk, u, w, q_dec, k_dec)) + (g_last.transpose(2, 0, 1),)
    state0 = jnp.zeros((b_sz, n_h, dk, dv), dtype=jnp.float32)
    _, o = lax.scan(step, state0, xs)
    return o.transpose(1, 0, 3, 2, 4).reshape(b_sz, s_len, n_h, dv)


def gated_deltanet(h, w_in, conv_w, a_log, dt_bias, norm_g, w_out):
    b_sz, s_len, _ = h.shape
    proj = h @ w_in
    qkv, z, b_gate, a_gate = jnp.split(
        proj, [2 * GDN_KEY_DIM + GDN_VALUE_DIM, 2 * GDN_KEY_DIM + 2 * GDN_VALUE_DIM,
               2 * GDN_KEY_DIM + 2 * GDN_VALUE_DIM + GDN_V_HEADS], axis=-1)
    qkv = jax.nn.silu(causal_dwconv(qkv, conv_w))
    q, k, v = jnp.split(qkv, [GDN_KEY_DIM, 2 * GDN_KEY_DIM], axis=-1)
    q = l2_norm(q.reshape(b_sz, s_len, GDN_QK_HEADS, GDN_HEAD_DIM).astype(jnp.float32))
    k = l2_norm(k.reshape(b_sz, s_len, GDN_QK_HEADS, GDN_HEAD_DIM).astype(jnp.float32))
    q = jnp.repeat(q, GDN_REPEAT, axis=2)
    k = jnp.repeat(k, GDN_REPEAT, axis=2)
    v = v.reshape(b_sz, s_len, GDN_V_HEADS, GDN_HEAD_DIM).astype(jnp.float32)
    beta = jax.nn.sigmoid(b_gate.astype(jnp.float32))
    g = -jnp.exp(a_log.astype(jnp.float32)) * jax.nn.softplus(a_gate.astype(jnp.float32) + dt_bias.astype(jnp.float32))
    o = gated_delta_rule(q, k, v, g, beta)
    o = rms_norm(o, norm_g) * jax.nn.silu(z.reshape(b_sz, s_len, GDN_V_HEADS, GDN_HEAD_DIM).astype(jnp.float32))
    return o.reshape(b_sz, s_len, GDN_VALUE_DIM).astype(h.dtype) @ w_out


def conv_ffn(h, w_up, conv_w, conv_b, w_down):
    u = causal_dwconv(h @ w_up, conv_w) + conv_b
    val, gate = jnp.split(u, 2, axis=-1)
    return (jax.nn.silu(gate) * val) @ w_down


def setup_inputs(seed: int = 0) -> dict:
    key = jax.random.key(seed)
    ks = jax.random.split(key, 24)
    f32 = jnp.float32

    def nrm(k, shape, scale):
        return jax.random.normal(k, shape, dtype=f32) * scale

    def gain(k, shape):
        return 1.0 + 0.02 * jax.random.normal(k, shape, dtype=f32)

    na, nb = N_ATTN_LAYERS, N_GDN_LAYERS
    dt = jnp.exp(jax.random.uniform(ks[16], (nb, GDN_V_HEADS), dtype=f32,
                                    minval=math.log(0.001), maxval=math.log(0.1)))
    return {
        'x': nrm(ks[0], (BATCH, SEQ, D_MODEL), 1.0),
        'mixer_norm': gain(ks[1], (DEPTH, D_MODEL)),
        'ffn_norm': gain(ks[2], (DEPTH, D_MODEL)),
        'diff_w_qkv': nrm(ks[3], (na, D_MODEL, 2 * DIFF_QK_DIM + DIFF_V_DIM), D_MODEL ** -0.5),
        'diff_q_norm': gain(ks[4], (na, DIFF_HEAD_DIM)),
        'diff_k_norm': gain(ks[5], (na, DIFF_HEAD_DIM)),
        'diff_lambda_q1': nrm(ks[6], (na, DIFF_HEAD_DIM), LAMBDA_STD),
        'diff_lambda_k1': nrm(ks[7], (na, DIFF_HEAD_DIM), LAMBDA_STD),
        'diff_lambda_q2': nrm(ks[8], (na, DIFF_HEAD_DIM), LAMBDA_STD),
        'diff_lambda_k2': nrm(ks[9], (na, DIFF_HEAD_DIM), LAMBDA_STD),
        'diff_subln': gain(ks[10], (na, 2 * DIFF_HEAD_DIM)),
        'diff_w_o': nrm(ks[11], (na, DIFF_V_DIM, D_MODEL), DIFF_V_DIM ** -0.5),
        'gdn_w_in': nrm(ks[12], (nb, D_MODEL, GDN_IN_DIM), D_MODEL ** -0.5),
        'gdn_conv_w': nrm(ks[13], (nb, GDN_CONV, 2 * GDN_KEY_DIM + GDN_VALUE_DIM), GDN_CONV ** -0.5),
        'gdn_A_log': jnp.log(jax.random.uniform(ks[14], (nb, GDN_V_HEADS), dtype=f32, minval=1.0, maxval=16.0)),
        'gdn_dt_bias': dt + jnp.log(-jnp.expm1(-dt)),
        'gdn_norm': gain(ks[15], (nb, GDN_HEAD_DIM)),
        'gdn_w_out': nrm(ks[17], (nb, GDN_VALUE_DIM, D_MODEL), GDN_VALUE_DIM ** -0.5),
        'ffn_w_up': nrm(ks[18], (DEPTH, D_MODEL, 2 * D_FF), D_MODEL ** -0.5),
        'ffn_conv_w': nrm(ks[19], (DEPTH, FFN_CONV, 2 * D_FF), FFN_CONV ** -0.5),
        'ffn_conv_b': nrm(ks[20], (DEPTH, 2 * D_FF), 0.02),
        'ffn_w_down': nrm(ks[21], (DEPTH, D_FF, D_MODEL), D_FF ** -0.5),
    }


def reference(x, mixer_norm, ffn_norm, diff_w_qkv, diff_q_norm, diff_k_norm, diff_lambda_q1,
              diff_lambda_k1, diff_lambda_q2, diff_lambda_k2, diff_subln, diff_w_o, gdn_w_in,
              gdn_conv_w, gdn_A_log, gdn_dt_bias, gdn_norm, gdn_w_out, ffn_w_up, ffn_conv_w,
              ffn_conv_b, ffn_w_down):
    for i in range(DEPTH):
        h = rms_norm(x, mixer_norm[i])
        j = i // N_MIXERS
        if i % N_MIXERS == 0:
            lambda_init = 0.8 - 0.6 * math.exp(-0.3 * i)
            x = x + diff_attention(h, diff_w_qkv[j], diff_q_norm[j], diff_k_norm[j], diff_lambda_q1[j],
                                   diff_lambda_k1[j], diff_lambda_q2[j], diff_lambda_k2[j],
                                   diff_subln[j], diff_w_o[j], lambda_init)
        else:
            x = x + gated_deltanet(h, gdn_w_in[j], gdn_conv_w[j], gdn_A_log[j], gdn_dt_bias[j],
                                   gdn_norm[j], gdn_w_out[j])
        x = x + conv_ffn(rms_norm(x, ffn_norm[i]), ffn_w_up[i], ffn_conv_w[i], ffn_conv_b[i], ffn_w_down[i])
    return x
```

```python
import contextlib
import math
import numpy as np
import concourse.bass as bass
import concourse.mybir as mybir
from concourse.bass_utils import run_bass_kernel_spmd

F32 = mybir.dt.float32
BF16 = mybir.dt.bfloat16
AF = mybir.ActivationFunctionType
ALU = mybir.AluOpType
AX = mybir.AxisListType

ENGS = ("pe", "act", "dve", "pool", "sp")
NCORES = 8
S = 8192
D = 2048
DFF = 5504
NF = 43
EPS = 1e-6


class T:
    __slots__ = ("h", "name", "w", "r")

    def __init__(self, h, name):
        self.h = h
        self.name = name
        self.w = None
        self.r = {}

    def __getitem__(self, idx):
        return self.h[idx]


class Prog:
    def __init__(self, nc):
        self.nc = nc
        self.ops = {e: [] for e in ENGS}
        self.cnt = {}
        self.seen = {e: {} for e in ENGS}
        self.sems = {}
        self.stack = None
        self.n_inst = 0
        self.n_wait = 0

    def sem(self, key):
        if key not in self.sems:
            self.sems[key] = self.stack.enter_context(self.nc.semaphore("s_" + key))
            self.cnt[key] = 0
        return key

    def sb(self, name, shape, dt):
        return T(self.stack.enter_context(self.nc.sbuf_tensor(name, list(shape), dt)), name)

    def ps(self, name, shape, dt=F32):
        return T(self.stack.enter_context(self.nc.psum_tensor(name, list(shape), dt)), name)

    def dram(self, name, shape, dt, kind=None):
        if kind is None:
            h = self.nc.dram_tensor(name, list(shape), dt)
        else:
            h = self.nc.dram_tensor(name, list(shape), dt, kind=kind)
        return T(h.ap(), name)

    def _need(self, eng, reads, writes):
        need = {}

        def add(k, v):
            if k == "pe" and eng == "pe":
                return
            if need.get(k, 0) < v:
                need[k] = v
        for t in reads:
            if t.w is not None:
                add(*t.w)
        for t in writes:
            if t.w is not None:
                add(*t.w)
            for k, v in t.r.items():
                if k == eng:
                    continue
                add(k, v)
        out = []
        seen = self.seen[eng]
        for k, v in need.items():
            if seen.get(k, 0) >= v:
                continue
            seen[k] = v
            out.append((k, v))
        return out

    def _mark(self, ev, reads, writes):
        k, v = ev
        for t in reads:
            if t.r.get(k, 0) < v:
                t.r[k] = v
        for t in writes:
            t.w = ev
            t.r = {}

    def op(self, eng, fn, reads=(), writes=()):
        waits = self._need(eng, reads, writes)
        self.sem(eng)
        self.cnt[eng] += 1
        ev = (eng, self.cnt[eng])
        self._mark(ev, reads, writes)
        self.ops[eng].append((waits, fn, (eng, 1)))
        self.n_inst += 1
        self.n_wait += len(waits)
        return ev

    def dma(self, q, out_t, out_ap, in_t, in_ap, semkey=None):
        if semkey is None:
            semkey = "d_" + out_t.name
        self.sem(semkey)
        waits = self._need(q, [in_t], [out_t])
        self.cnt[semkey] += 16
        ev = (semkey, self.cnt[semkey])
        self._mark(ev, [in_t], [out_t])

        def fn(e, out_ap=out_ap, in_ap=in_ap):
            return e.dma_start(out=out_ap, in_=in_ap)
        self.ops[q].append((waits, fn, (semkey, 16)))
        self.n_inst += 1
        self.n_wait += len(waits)
        return ev

    def wait_all(self, eng, tiles):
        waits = self._need(eng, tiles, [])
        self.ops[eng].append((waits, None, None))

    def replay(self):
        nc, sems, ops = self.nc, self.sems, self.ops
        with nc.Block() as block:
            def run(e, lst):
                for waits, fn, inc in lst:
                    for k, v in waits:
                        e.wait_ge(sems[k], v)
                    if fn is not None:
                        fn(e).then_inc(sems[inc[0]], inc[1])

            @block.tensor
            def _(e):
                run(e, ops["pe"])

            @block.scalar
            def _(e):
                run(e, ops["act"])

            @block.vector
            def _(e):
                run(e, ops["dve"])

            @block.gpsimd
            def _(e):
                run(e, ops["pool"])

            @block.sync
            def _(e):
                run(e, ops["sp"])

    def mm(self, ot, oap, lt, lap, rt, rap, start=True, stop=True):
        self.op("pe", lambda e: e.matmul(oap, lap, rap, start=start, stop=stop), [lt, rt], [ot])

    def tr(self, ot, oap, it, iap, idt, idap):
        self.op("pe", lambda e: e.transpose(oap, iap, idap), [it, idt], [ot])

    def act(self, ot, oap, it, iap, func, reads=(), **kw):
        self.op("act", lambda e: e.activation(oap, iap, func, **kw), [it] + list(reads), [ot])

    def ts(self, eng, ot, oap, it, iap, s1, s2, op0, op1=None, reads=()):
        if op1 is None:
            self.op(eng, lambda e: e.tensor_scalar(oap, iap, s1, s2, op0), [it] + list(reads), [ot])
        else:
            self.op(eng, lambda e: e.tensor_scalar(oap, iap, s1, s2, op0, op1), [it] + list(reads), [ot])

    def tt(self, eng, ot, oap, at, aap, bt, bap, op):
        self.op(eng, lambda e: e.tensor_tensor(oap, aap, bap, op), [at, bt], [ot])

    def stt(self, eng, ot, oap, at, aap, scalar, bt, bap, op0, op1, reads=()):
        self.op(eng, lambda e: e.scalar_tensor_tensor(oap, aap, scalar, bap, op0, op1),
                [at, bt] + list(reads), [ot])

    def cp(self, eng, ot, oap, it, iap):
        if eng == "act":
            self.op("act", lambda e: e.activation(oap, iap, AF.Copy), [it], [ot])
        else:
            self.op(eng, lambda e: e.tensor_copy(oap, iap), [it], [ot])

    def memset(self, eng, ot, oap, val):
        self.op(eng, lambda e: e.memset(oap, val), [], [ot])


class NormT:
    def __init__(self, P, ident, gn, tag="", tp=None):
        self.P = P
        self.ident = ident
        self.gn = gn
        self.junk = P.sb("nt_junk" + tag, [128, D], BF16)
        self.xs = [P.sb("nt_xs%d%s" % (i, tag), [128, D], BF16) for i in range(2)]
        self.ss = [P.sb("nt_ss%d%s" % (i, tag), [128, 2], F32) for i in range(2)]
        self.tp = tp if tp is not None else [P.ps("nt_tp%d%s" % (i, tag), [128, 4, 128], BF16) for i in range(2)]
        self.k = 0
        self.j = 0

    def run(self, xt, xap, rows, hT, hT_ap_fn):
        P = self.P
        k = self.k
        self.k ^= 1
        ss, xs = self.ss[k], self.xs[k]
        P.memset("pool", ss, ss[:, :], 0.0)
        P.op("act", lambda e: e.activation(self.junk[0:rows, :], xap, AF.Square,
                                           accum_out=ss[0:rows, 0:1]), [xt, ss], [self.junk, ss])
        P.ts("dve", ss, ss[0:rows, 1:2], ss, ss[0:rows, 0:1], 1.0 / D, EPS, ALU.mult, ALU.add)
        P.act(ss, ss[0:rows, 1:2], ss, ss[0:rows, 1:2], AF.Sqrt)
        P.op("dve", lambda e: e.reciprocal(ss[0:rows, 1:2], ss[0:rows, 1:2]), [ss], [ss])
        P.ts("dve", xs, xs[0:rows, :], xt, xap, ss[0:rows, 1:2], None, ALU.mult, reads=[ss])
        for cg in range(4):
            tp = self.tp[self.j]
            self.j ^= 1
            for j in range(4):
                c = cg * 4 + j
                P.tr(tp, tp[:, j, 0:rows], xs, xs[0:rows, c * 128:(c + 1) * 128],
                     self.ident, self.ident[0:rows, 0:rows])
            for j in range(4):
                c = cg * 4 + j
                if j % 2 == 0:
                    P.op("act", lambda e, c=c, j=j, tp=tp: e.activation(
                        hT_ap_fn(c), tp[:, j, 0:rows], AF.Copy, scale=self.gn[:, c:c + 1]),
                        [tp, self.gn], [hT])
                else:
                    P.ts("dve", hT, hT_ap_fn(c), tp, tp[:, j, 0:rows], self.gn[:, c:c + 1], None,
                         ALU.mult, reads=[self.gn])


def load_consts(P, ident_d, gn_d):
    idf = P.sb("idf", [128, 128], F32)
    ident = P.sb("ident_sb", [128, 128], BF16)
    gn = P.sb("gn_sb", [128, 16], F32)
    P.dma("sp", idf, idf[:, :], ident_d, ident_d[:, :])
    P.dma("sp", gn, gn[:, :], gn_d, gn_d[:, :])
    P.cp("dve", ident, ident[:, :], idf, idf[:, :])
    return ident, gn


class WStream:
    def __init__(self, P, wst, wbf):
        self.P, self.wst, self.wbf, self.i = P, wst, wbf, 0

    def load(self, src_t, src_ap, shape3):
        P = self.P
        i = self.i
        self.i += 1
        a, b = shape3
        stg = self.wst[i % len(self.wst)]
        wb = self.wbf[i % len(self.wbf)]
        sv = stg[:, 0:a * b].rearrange("p (a b) -> p a b", a=a)
        P.dma("sp", stg, sv, src_t, src_ap)
        eng = "pool" if i % 2 == 0 else "act"
        P.cp(eng, wb, wb[:, 0:a * b], stg, stg[:, 0:a * b])
        return wb, wb[:, 0:a * b].rearrange("p (a b) -> p a b", a=a)


def down_proj(P, ws, gT, nchunk, w_d, w_v, pm, xres, yo, x_d, y_d, base, oi):
    ngrp = (nchunk + 7) // 8
    for n in range(4):
        for grp in range(ngrp):
            f0 = grp * 8
            nf = min(8, nchunk - f0)
            wd_t, wd = ws.load(w_d, w_v[:, f0:f0 + nf, n * 512:(n + 1) * 512], (nf, 512))
            for t in range(4):
                for fl in range(nf):
                    f = f0 + fl
                    P.mm(pm[t], pm[t][:, :], gT, gT[:, f, t * 128:(t + 1) * 128],
                         wd_t, wd[:, fl, :], start=(f == 0), stop=(f == nchunk - 1))
        for t in range(4):
            r0 = base + t * 128
            xr = xres[oi[0] % 2]
            yt = yo[oi[0] % 2]
            oi[0] += 1
            P.dma("sp", xr, xr[:, :], x_d, x_d[r0:r0 + 128, n * 512:(n + 1) * 512])
            P.tt("dve", yt, yt[:, :], pm[t], pm[t][:, :], xr, xr[:, :], ALU.add)
            P.dma("pool", y_d, y_d[r0:r0 + 128, n * 512:(n + 1) * 512], yt, yt[:, :],
                  semkey="st%d" % (oi[0] % 2))


def build_oproj(K):
    nck = K // 128
    nc = bass.Bass("TRN2", target_bir_lowering=False)
    P = Prog(nc)
    a_d = P.dram("a", [1024, K], F32, "ExternalInput")
    x_d = P.dram("x", [1024, D], F32, "ExternalInput")
    id_d = P.dram("ident", [128, 128], F32, "ExternalInput")
    w_d = P.dram("w", [K, D], F32, "ExternalInput")
    y_d = P.dram("y", [1024, D], F32, "ExternalOutput")
    w_v = w_d.h.rearrange("(f p) n -> p f n", p=128)
    with contextlib.ExitStack() as st:
        P.stack = st
        idf = P.sb("idf", [128, 128], F32)
        ident = P.sb("ident_sb", [128, 128], BF16)
        P.dma("sp", idf, idf[:, :], id_d, id_d[:, :])
        P.cp("dve", ident, ident[:, :], idf, idf[:, :])
        aT = P.sb("aT", [128, nck, 512], BF16)
        ab = [P.sb("ab%d" % i, [128, K], F32) for i in range(2)]
        abf = [P.sb("abf%d" % i, [128, K], BF16) for i in range(2)]
        wst = [P.sb("wst%d" % i, [128, 4096], F32) for i in range(2)]
        wbf = [P.sb("wbf%d" % i, [128, 4096], BF16) for i in range(4)]
        xres = [P.sb("xres%d" % i, [128, 512], F32) for i in range(2)]
        yo = [P.sb("yo%d" % i, [128, 512], F32) for i in range(2)]
        pm = [P.ps("pm%d" % i, [128, 512]) for i in range(4)]
        tp = [P.ps("tp%d" % i, [128, 4, 128], BF16) for i in range(2)]
        ws = WStream(P, wst, wbf)
        oi = [0]
        ti = 0
        for half in range(2):
            base = half * 512
            for t in range(4):
                at, abt = ab[t % 2], abf[t % 2]
                P.dma("sp", at, at[:, :], a_d, a_d[base + t * 128:base + (t + 1) * 128, :])
                P.cp("dve", abt, abt[:, :], at, at[:, :])
                for cg in range(nck // 4):
                    tpt = tp[ti % 2]
                    ti += 1
                    for j in range(4):
                        c = cg * 4 + j
                        P.tr(tpt, tpt[:, j, :], abt, abt[:, c * 128:(c + 1) * 128], ident, ident[:, :])
                    P.cp("act", aT, aT[:, cg * 4:(cg + 1) * 4, t * 128:(t + 1) * 128], tpt, tpt[:, :, :])
            down_proj(P, ws, aT, nck, w_d, w_v, pm, xres, yo, x_d, y_d, base, oi)
        P.wait_all("pool", [y_d])
        P.replay()
    return nc


def run_oproj(a, x, w):
    K = a.shape[1]
    nc = build_oproj(K)
    ident = np.eye(128, dtype=np.float32)
    in_maps = [{"a": np.ascontiguousarray(a[c * 1024:(c + 1) * 1024]),
                "x": np.ascontiguousarray(x[c * 1024:(c + 1) * 1024]),
                "ident": ident, "w": w} for c in range(NCORES)]
    res = run_bass_kernel_spmd(nc, in_maps, core_ids=list(range(NCORES)))
    return np.concatenate([res.results[c]["y"] for c in range(NCORES)], axis=0)


def build_ffn():
    nc = bass.Bass("TRN2", target_bir_lowering=False)
    P = Prog(nc)
    x_d = P.dram("x", [1024, D], F32, "ExternalInput")
    xh_d = P.dram("xh", [2, D], F32, "ExternalInput")
    id_d = P.dram("ident", [128, 128], F32, "ExternalInput")
    gn_d = P.dram("gn", [128, 16], F32, "ExternalInput")
    wup_d = P.dram("wup", [D, 2 * DFF], F32, "ExternalInput")
    cw_d = P.dram("cw", [128, 86 * 3], F32, "ExternalInput")
    cb_d = P.dram("cb", [128, 86], F32, "ExternalInput")
    wdn_d = P.dram("wdn", [DFF, D], F32, "ExternalInput")
    y_d = P.dram("y", [1024, D], F32, "ExternalOutput")
    wup_v = wup_d.h.rearrange("(c p) n -> p c n", p=128)
    wdn_v = wdn_d.h.rearrange("(f p) n -> p f n", p=128)
    with contextlib.ExitStack() as st:
        P.stack = st
        ident, gn = load_consts(P, id_d, gn_d)
        cw = P.sb("cw_sb", [128, 86 * 3], F32)
        cb = P.sb("cb_sb", [128, 86], F32)
        P.dma("sp", cw, cw[:, :], cw_d, cw_d[:, :])
        P.dma("sp", cb, cb[:, :], cb_d, cb_d[:, :])
        hT = P.sb("hT", [128, 16, 1026], BF16)
        gT = P.sb("gT", [128, NF, 512], BF16)
        xb = [P.sb("xb%d" % i, [128, D], F32) for i in range(2)]
        wst = [P.sb("wst%d" % i, [128, 4096], F32) for i in range(2)]
        wbf = [P.sb("wbf%d" % i, [128, 4096], BF16) for i in range(4)]
        u = [P.sb("u%d" % i, [128, 514], F32) for i in range(2)]
        yv = P.sb("yv", [128, 512], F32)
        yg = P.sb("yg", [128, 512], F32)
        sg = P.sb("sg", [128, 512], F32)
        xres = [P.sb("xres%d" % i, [128, 512], F32) for i in range(2)]
        yo = [P.sb("yo%d" % i, [128, 512], F32) for i in range(2)]
        pm = [P.ps("pm%d" % i, [128, 512]) for i in range(4)]
        ptail = [P.ps("ptail%d" % i, [128, 2]) for i in range(2)]

        nt = NormT(P, ident, gn)
        for t in range(9):
            xt = xb[t % 2]
            if t == 0:
                rows, col0 = 2, 0
                P.dma("sp", xt, xt[0:2, :], xh_d, xh_d[:, :])
            else:
                rows, col0 = 128, 2 + (t - 1) * 128
                P.dma("sp", xt, xt[:, :], x_d, x_d[(t - 1) * 128:t * 128, :])
            nt.run(xt, xt[0:rows, :], rows, hT,
                   lambda c, col0=col0, rows=rows: hT[:, c, col0:col0 + rows])

        ws = WStream(P, wst, wbf)

        oi = [0]
        for half in range(2):
            base = half * 512
            ui = 0
            for blk in range(22):
                nch = 2 if blk < 21 else 1
                wcols = nch * 128
                wv_t, wv = ws.load(wup_d, wup_v[:, :, blk * 256:blk * 256 + wcols], (16, wcols))
                wg_t, wg = ws.load(wup_d, wup_v[:, :, DFF + blk * 256:DFF + blk * 256 + wcols], (16, wcols))
                for i in range(nch):
                    f = blk * 2 + i
                    ys = []
                    for part, (w_t, w_v, ydst) in enumerate(((wv_t, wv, yv), (wg_t, wg, yg))):
                        ch = f + part * NF
                        pmain = pm[ui % 4]
                        ptl = ptail[ui % 2]
                        ut = u[ui % 2]
                        ui += 1
                        for c in range(16):
                            P.mm(pmain, pmain[:, :], w_t, w_v[:, c, i * 128:(i + 1) * 128],
                                 hT, hT[:, c, base:base + 512], start=(c == 0), stop=(c == 15))
                        for c in range(16):
                            P.mm(ptl, ptl[:, :], w_t, w_v[:, c, i * 128:(i + 1) * 128],
                                 hT, hT[:, c, base + 512:base + 514], start=(c == 0), stop=(c == 15))
                        P.cp("act", ut, ut[:, 0:512], pmain, pmain[:, :])
                        P.cp("act", ut, ut[:, 512:514], ptl, ptl[:, :])
                        P.ts("dve", ydst, ydst[:, :], ut, ut[:, 2:514], cw[:, ch * 3 + 2:ch * 3 + 3],
                             cb[:, ch:ch + 1], ALU.mult, ALU.add, reads=[cw, cb])
                        P.stt("dve", ydst, ydst[:, :], ut, ut[:, 1:513], cw[:, ch * 3 + 1:ch * 3 + 2],
                              ydst, ydst[:, :], ALU.mult, ALU.add, reads=[cw])
                        P.stt("dve", ydst, ydst[:, :], ut, ut[:, 0:512], cw[:, ch * 3:ch * 3 + 1],
                              ydst, ydst[:, :], ALU.mult, ALU.add, reads=[cw])
                    P.act(sg, sg[:, :], yg, yg[:, :], AF.Silu)
                    P.tt("dve", gT, gT[:, f, :], yv, yv[:, :], sg, sg[:, :], ALU.mult)
            down_proj(P, ws, gT, NF, wdn_d, wdn_v, pm, xres, yo, x_d, y_d, base, oi)
        P.wait_all("pool", [y_d])
        P.replay()
    return nc


def _gn_layout(g):
    return np.ascontiguousarray(g.reshape(16, 128).T.astype(np.float32))


def run_ffn(x, norm_g, w_up, conv_w, conv_b, w_down):
    nc = build_ffn()
    ident = np.eye(128, dtype=np.float32)
    gn = _gn_layout(norm_g)
    cw = np.ascontiguousarray(conv_w.reshape(3, 86, 128).transpose(2, 1, 0).reshape(128, 86 * 3))
    cb = np.ascontiguousarray(conv_b.reshape(86, 128).T)
    in_maps = []
    for c in range(NCORES):
        xs = np.ascontiguousarray(x[c * 1024:(c + 1) * 1024])
        xh = np.zeros((2, D), np.float32) if c == 0 else np.ascontiguousarray(x[c * 1024 - 2:c * 1024])
        in_maps.append({"x": xs, "xh": xh, "ident": ident, "gn": gn, "wup": w_up, "cw": cw, "cb": cb,
                        "wdn": w_down})
    res = run_bass_kernel_spmd(nc, in_maps, core_ids=list(range(NCORES)))
    return np.concatenate([res.results[c]["y"] for c in range(NCORES)], axis=0)


LAMBDA_INIT0 = 0.8 - 0.6 * math.exp(-0.3 * 0)
NQB = S // 128
ATT_NQB_RUN = NQB
ATT_NG_RUN = S // 512
ATT_DBG = 99
ATT_DBGOUT = False
ATT_TRACE = False


def build_att():
    nc = bass.Bass("TRN2", target_bir_lowering=False)
    P = Prog(nc)
    x_d = P.dram("x", [S, D], F32, "ExternalInput")
    id_d = P.dram("ident", [128, 128], F32, "ExternalInput")
    gn_d = P.dram("gn", [128, 16], F32, "ExternalInput")
    wq_d = P.dram("wq", [D, 256], F32, "ExternalInput")
    wk_d = P.dram("wk", [D, 256], F32, "ExternalInput")
    wv_d = P.dram("wv", [D, 256], F32, "ExternalInput")
    qkn_d = P.dram("qkn", [128, 2], F32, "ExternalInput")
    lam_d = P.dram("lamv", [128, 4, 128], F32, "ExternalInput")
    sub_d = P.dram("subg", [128, 256], F32, "ExternalInput")
    db_d = P.dram("dbias", [128, 128], F32, "ExternalInput")
    br_d = P.dram("brow", [2, S], F32, "ExternalInput")
    o_d = P.dram("o", [S, 256], F32, "ExternalOutput")
    dbg_d = P.dram("dbg", [128, 2048], F32, "ExternalOutput") if ATT_DBGOUT else None
    with contextlib.ExitStack() as st:
        P.stack = st
        V = P.sb("V", [128, NQB, 272], BF16)
        ident, gn = load_consts(P, id_d, gn_d)
        P.memset("dve", V, V[:, :, 256:257], 1.0)
        qkn = P.sb("qkn_sb", [128, 2], F32)
        P.dma("sp", qkn, qkn[:, :], qkn_d, qkn_d[:, :])
        lamv = P.sb("lamv_sb", [128, 4, 128], F32)
        P.dma("sp", lamv, lamv[:, :, :], lam_d, lam_d[:, :, :])
        subg = P.sb("subg_sb", [128, 256], F32)
        P.dma("sp", subg, subg[:, :], sub_d, sub_d[:, :])
        dbias = P.sb("dbias_sb", [128, 128], F32)
        P.dma("sp", dbias, dbias[:, :], db_d, db_d[:, :])
        brow = P.sb("brow_sb", [2, S], BF16)
        wstg = P.sb("wstg", [128, 16, 256], F32)
        wflat = wstg[0:2, :, :].rearrange("p a b -> p (a b)")
        for hb in range(2):
            P.dma("sp", wstg, wflat, br_d, br_d[:, hb * 4096:(hb + 1) * 4096])
            P.cp("dve", brow, brow[:, hb * 4096:(hb + 1) * 4096], wstg, wflat)
        ones = P.sb("ones", [128, 128], BF16)
        P.memset("pool", ones, ones[:, :], 1.0)
        lt = P.sb("lt", [128, 2, 128], F32)
        ls = P.sb("ls", [128, 4], F32)
        P.tt("dve", lt, lt[:, 0, :], lamv, lamv[:, 0, :], lamv, lamv[:, 1, :], ALU.mult)
        P.tt("dve", lt, lt[:, 1, :], lamv, lamv[:, 2, :], lamv, lamv[:, 3, :], ALU.mult)
        P.op("dve", lambda e: e.reduce_sum(ls[:, 0:2], lt[:, :, :], AX.X), [lt], [ls])
        P.act(ls, ls[:, 0:2], ls, ls[:, 0:2], AF.Exp)
        P.tt("dve", ls, ls[:, 2:3], ls, ls[:, 0:1], ls, ls[:, 1:2], ALU.subtract)
        P.ts("dve", ls, ls[:, 3:4], ls, ls[:, 2:3], LAMBDA_INIT0, -1.0, ALU.add, ALU.mult)
        P.ts("dve", subg, subg[:, :], subg, subg[:, :], 1.0 - LAMBDA_INIT0, None, ALU.mult)

        wb = []
        for i, wd in enumerate((wq_d, wk_d, wv_d)):
            w = P.sb("wb%d" % i, [128, 16, 256], BF16)
            P.dma("sp", wstg, wstg[:, :, :], wd, wd.h.rearrange("(c p) n -> p c n", p=128))
            P.cp("pool", w, w[:, :, :], wstg, wstg[:, :, :])
            wb.append(w)
        qT = P.sb("qT", [128, 2, S], BF16)
        kT = P.sb("kT", [128, 2, S], BF16)
        hTg = P.sb("hTg", [128, 16, 512], BF16)
        xb = [P.sb("xb0", [128, D], F32)]
        sq = P.sb("sq", [128, 512], BF16)
        rr = P.sb("rr", [128, 512], F32)
        pA = [P.ps("pA%d" % i, [128, 512]) for i in range(2)]
        pS = P.ps("pS", [128, 512])
        pVb = P.ps("pVb", [128, 512])
        pV = [T(pVb.h[:, 0:256], "pV0"), T(pVb.h[:, 256:512], "pV1")]
        pO = [P.ps("pO%d" % i, [128, 512]) for i in range(2)]

        nt = NormT(P, ident, gn)
        ai = 0
        for g in range(ATT_NG_RUN):
            for j in range(4):
                t = g * 4 + j
                xt = xb[0]
                P.dma("sp", xt, xt[:, :], x_d, x_d[t * 128:(t + 1) * 128, :])
                nt.run(xt, xt[:, :], 128, hTg, lambda c, j=j: hTg[:, c, j * 128:(j + 1) * 128])
            for which, (w, dst) in enumerate(((wb[0], qT), (wb[1], kT))):
                for m in range(2):
                    pa = pA[ai % 2]
                    ai += 1
                    for c in range(16):
                        P.mm(pa, pa[:, :], w, w[:, c, m * 128:(m + 1) * 128], hTg, hTg[:, c, :],
                             start=(c == 0), stop=(c == 15))
                    if ATT_DBG < 2:
                        continue
                    P.act(sq, sq[:, :], pa, pa[:, :], AF.Square)
                    if ATT_DBG < 3:
                        continue
                    P.mm(pS, pS[:, :], ones, ones[:, :], sq, sq[:, :])
                    if which == 0:
                        P.ts("dve", rr, rr[:, :], pS, pS[:, :], 1.0, 128.0 * EPS, ALU.mult, ALU.add)
                    else:
                        P.ts("dve", rr, rr[:, :], pS, pS[:, :], 1.0 / 128.0, EPS, ALU.mult, ALU.add)
                    if ATT_DBG < 4:
                        continue
                    P.act(rr, rr[:, :], rr, rr[:, :], AF.Ln)
                    P.act(rr, rr[:, :], rr, rr[:, :], AF.Exp, scale=-0.5)
                    if ATT_DBG < 5:
                        continue
                    P.stt("dve", dst, dst[:, m, g * 512:(g + 1) * 512], pa, pa[:, :],
                          qkn[:, which:which + 1], rr, rr[:, :], ALU.mult, ALU.mult, reads=[qkn])
            for j in range(4):
                t = g * 4 + j
                pv = pA[ai % 2]
                ai += 1
                for c in range(16):
                    P.mm(pv, pv[:, 0:256], hTg, hTg[:, c, j * 128:(j + 1) * 128], wb[2], wb[2][:, c, :],
                         start=(c == 0), stop=(c == 15))
                P.cp("act", V, V[:, t, 0:256], pv, pv[:, 0:256])

        if ATT_DBGOUT:
            dbg = xb[0]
            P.cp("dve", dbg, dbg[:, 0:128], qT, qT[:, 0, 0:128])
            P.cp("dve", dbg, dbg[:, 128:256], kT, kT[:, 0, 0:128])
            P.cp("dve", dbg, dbg[:, 256:514], V, V[:, 0, 0:258])
            P.cp("dve", dbg, dbg[:, 520:524], ls, ls[:, 0:4])
            P.cp("dve", dbg, dbg[:, 600:728], qT, qT[:, 1, 0:128])
            P.cp("dve", dbg, dbg[:, 728:856], kT, kT[:, 1, 0:128])
        pT = [P.sb("pT%d" % i, [128, 4, 128], BF16) for i in range(2)]
        sd = P.sb("sd", [128, 128], F32)
        pTd = P.sb("pTd", [128, 128], BF16)
        st4 = [P.sb("st4_%d" % i, [128, 8], F32) for i in range(2)]
        o1 = [P.sb("o1_%d" % i, [128, 256], F32) for i in range(2)]
        od = [P.sb("od_%d" % i, [128, 256], F32) for i in range(2)]
        oo = [P.sb("oo_%d" % i, [128, 256], F32) for i in range(2)]
        junk = nt.junk
        gi = 0
        for qi in range(ATT_NQB_RUN):
            qs = slice(qi * 128, (qi + 1) * 128)
            for m in range(2):
                po = pO[m]
                first = True
                for g0 in range(0, qi, 4):
                    n = min(4, qi - g0)
                    sc = pA[gi % 2]
                    pt = pT[gi % 2]
                    gi += 1
                    scv = sc[:, :].rearrange("p (a b) -> p a b", a=4)
                    for j in range(n):
                        kt = g0 + j
                        mi = qi - kt
                        P.mm(sc, scv[:, j, :], kT, kT[:, m, kt * 128:(kt + 1) * 128], qT, qT[:, m, qs],
                             start=True, stop=False)
                        P.mm(sc, scv[:, j, :], brow, brow[0:2, mi * 128:(mi + 1) * 128], ones, ones[0:2, :],
                             start=False, stop=True)
                    P.act(pt, pt[:, 0:n, :], sc, scv[:, 0:n, :], AF.Exp)
                    for j in range(n):
                        kt = g0 + j
                        P.mm(po, po[:, 0:257], pt, pt[:, j, :], V, V[:, kt, 0:257], start=first, stop=False)
                        first = False
                P.mm(pS, pS[:, 0:128], kT, kT[:, m, qs], qT, qT[:, m, qs])
                P.tt("dve", sd, sd[:, :], pS, pS[:, 0:128], dbias, dbias[:, :], ALU.add)
                P.act(pTd, pTd[:, :], sd, sd[:, :], AF.Exp)
                P.mm(po, po[:, 0:257], pTd, pTd[:, :], V, V[:, qi, 0:257], start=first, stop=True)
            if ATT_DBGOUT and qi == 0:
                P.cp("dve", dbg, dbg[:, 900:1158], pO[0], pO[0][:, 0:258])
                P.cp("dve", dbg, dbg[:, 1200:1458], pO[1], pO[1][:, 0:258])
                P.cp("dve", dbg, dbg[:, 1500:1628], pTd, pTd[:, :])
                P.cp("dve", dbg, dbg[:, 1700:1828], sd, sd[:, :])
                P.dma("sp", dbg_d, dbg_d[:, :], dbg, dbg[:, :])
                P.wait_all("sp", [dbg_d])
            k = qi % 2
            s4, o1t, odt, oot = st4[k], o1[k], od[k], oo[k]
            P.op("dve", lambda e, s4=s4: e.reciprocal(s4[:, 0:1], pO[0][:, 256:257]), [pO[0]], [s4])
            P.op("dve", lambda e, s4=s4: e.reciprocal(s4[:, 1:2], pO[1][:, 256:257]), [pO[1]], [s4])
            P.tt("dve", s4, s4[:, 2:3], s4, s4[:, 1:2], ls, ls[:, 3:4], ALU.mult)
            P.op("act", lambda e, s4=s4, o1t=o1t: e.activation(o1t[:, :], pO[0][:, 0:256], AF.Copy,
                                                               scale=s4[:, 0:1]), [pO[0], s4], [o1t])
            P.stt("dve", odt, odt[:, :], pO[1], pO[1][:, 0:256], s4[:, 2:3], o1t, o1t[:, :],
                  ALU.mult, ALU.add, reads=[s4])
            P.memset("pool", s4, s4[:, 4:5], 0.0)
            P.op("act", lambda e, s4=s4, odt=odt: e.activation(junk[:, 0:256], odt[:, :], AF.Square,
                                                               accum_out=s4[:, 4:5]), [odt, s4], [junk, s4])
            P.ts("dve", s4, s4[:, 5:6], s4, s4[:, 4:5], 1.0 / 256.0, EPS, ALU.mult, ALU.add)
            P.act(s4, s4[:, 5:6], s4, s4[:, 5:6], AF.Ln)
            P.act(s4, s4[:, 5:6], s4, s4[:, 5:6], AF.Exp, scale=-0.5)
            P.stt("dve", oot, oot[:, :], odt, odt[:, :], s4[:, 5:6], subg, subg[:, :],
                  ALU.mult, ALU.mult, reads=[s4])
            P.dma("pool", o_d, o_d[qs, :], oot, oot[:, :], semkey="st%d" % k)
        P.wait_all("pool", [o_d])
        P.replay()
    return nc


def _bf16_round(v):
    u = v.astype(np.float32).view(np.uint32).astype(np.uint64)
    u = ((u + 0x7FFF + ((u >> 16) & 1)) & 0xFFFF0000).astype(np.uint32)
    return u.view(np.float32)


def att_consts(head):
    slope = 2.0 ** (-8.0 * (head + 1) / 8.0)
    i = np.arange(128, dtype=np.float64)[:, None]
    j = np.arange(128, dtype=np.float64)[None, :]
    db = -slope * np.abs(j - i) + slope * (j - 64.0)
    allowed = (i < 64) | (j >= 64)
    db = np.where(allowed, db, -30000.0).astype(np.float32)
    m = np.arange(64, dtype=np.float64)[:, None]
    ii = np.arange(128, dtype=np.float64)[None, :]
    v = (slope * (ii - 64.0 - 128.0 * m)).reshape(-1).astype(np.float32)
    hi = _bf16_round(v)
    lo = (v - hi).astype(np.float32)
    return db, np.stack([hi, lo]).astype(np.float32)


def run_att(x, inp):
    nc = build_att()
    ident = np.eye(128, dtype=np.float32)
    gn = _gn_layout(inp["mixer_norm"][0])
    wqkv = inp["diff_w_qkv"][0]
    qkn = np.stack([inp["diff_q_norm"][0], inp["diff_k_norm"][0]], axis=1).astype(np.float32)
    lamv = np.stack([inp["diff_lambda_q1"][0], inp["diff_lambda_k1"][0],
                     inp["diff_lambda_q2"][0], inp["diff_lambda_k2"][0]])
    lamv = np.ascontiguousarray(np.broadcast_to(lamv[None], (128, 4, 128))).astype(np.float32)
    subg = np.ascontiguousarray(np.broadcast_to(inp["diff_subln"][0][None], (128, 256))).astype(np.float32)
    in_maps = []
    for c in range(NCORES):
        db, br = att_consts(c)
        in_maps.append({
            "x": x, "ident": ident, "gn": gn,
            "wq": np.ascontiguousarray(wqkv[:, c * 256:(c + 1) * 256]),
            "wk": np.ascontiguousarray(wqkv[:, 2048 + c * 256:2048 + (c + 1) * 256]),
            "wv": np.ascontiguousarray(wqkv[:, 4096 + c * 256:4096 + (c + 1) * 256]),
            "qkn": qkn, "lamv": lamv, "subg": subg, "dbias": db, "brow": br})
    res = run_bass_kernel_spmd(nc, in_maps, core_ids=list(range(NCORES)), trace=ATT_TRACE)
    if ATT_TRACE:
        print("ATT exec_time_ns", res.exec_time_ns)
    if ATT_DBGOUT:
        np.save("_att_dbg.npy", np.stack([res.results[c]["dbg"] for c in range(NCORES)]))
    return np.concatenate([res.results[c]["o"] for c in range(NCORES)], axis=1)


GDN_NG_RUN = S // 512
GDN_DBG = 99


def build_gdn():
    nc = bass.Bass("TRN2", target_bir_lowering=False)
    P = Prog(nc)
    x_d = P.dram("x", [S, D], F32, "ExternalInput")
    id_d = P.dram("ident", [128, 128], F32, "ExternalInput")
    gn_d = P.dram("gn", [128, 16], F32, "ExternalInput")
    wf_d = P.dram("wf", [D, 1024], F32, "ExternalInput")
    wz_d = P.dram("wz", [D, 512], F32, "ExternalInput")
    wba_d = P.dram("wba", [D, 8], F32, "ExternalInput")
    cw_d = P.dram("cw", [128, 32], F32, "ExternalInput")
    hp_d = P.dram("hp", [128, 8], F32, "ExternalInput")
    gno_d = P.dram("gno", [128, 128], F32, "ExternalInput")
    mk_d = P.dram("masks", [128, 2, 128], F32, "ExternalInput")
    o_d = P.dram("o", [S, 512], F32, "ExternalOutput")
    with contextlib.ExitStack() as st:
        P.stack = st
        ident, gn = load_consts(P, id_d, gn_d)
        idf = P.sb("idf2", [128, 128], F32)
        P.dma("sp", idf, idf[:, :], id_d, id_d[:, :])
        cw = P.sb("cw_sb", [128, 32], F32)
        P.dma("sp", cw, cw[:, :], cw_d, cw_d[:, :])
        hp = P.sb("hp_sb", [128, 8], F32)
        P.dma("sp", hp, hp[:, :], hp_d, hp_d[:, :])
        gno = P.sb("gno_sb", [128, 128], F32)
        P.dma("sp", gno, gno[:, :], gno_d, gno_d[:, :])
        mk = P.sb("mk_sb", [128, 2, 128], F32)
        P.dma("sp", mk, mk[:, :, :], mk_d, mk_d[:, :, :])
        Mle, Mgt = mk[:, 0, :], mk[:, 1, :]
        ones32 = P.sb("ones32", [128, 128], F32)
        P.memset("dve", ones32, ones32[:, :], 1.0)
        onesb = P.sb("onesb", [128, 128], BF16)
        P.memset("dve", onesb, onesb[:, :], 1.0)
        nA = P.sb("nA", [128, 4], F32)
        P.act(nA, nA[:, :], hp, hp[:, 0:4], AF.Exp)
        P.ts("dve", nA, nA[:, :], nA, nA[:, :], -1.0, None, ALU.mult)

        wstg = P.sb("wstg", [128, 16, 256], F32)
        wfb = P.sb("wfb", [128, 16, 1024], BF16)
        wzb = P.sb("wzb", [128, 16, 512], BF16)
        wbab = P.sb("wbab", [128, 16, 8], BF16)
        for i in range(4):
            P.dma("sp", wstg, wstg[:, :, :], wf_d,
                  wf_d.h.rearrange("(c p) n -> p c n", p=128)[:, :, i * 256:(i + 1) * 256])
            P.cp("act" if i % 2 else "dve", wfb, wfb[:, :, i * 256:(i + 1) * 256], wstg, wstg[:, :, :])
        for i in range(2):
            P.dma("sp", wstg, wstg[:, :, :], wz_d,
                  wz_d.h.rearrange("(c p) n -> p c n", p=128)[:, :, i * 256:(i + 1) * 256])
            P.cp("act" if i % 2 else "dve", wzb, wzb[:, :, i * 256:(i + 1) * 256], wstg, wstg[:, :, :])
        P.dma("sp", wstg, wstg[:, :, 0:8], wba_d, wba_d.h.rearrange("(c p) n -> p c n", p=128))
        P.cp("dve", wbab, wbab[:, :, :], wstg, wstg[:, :, 0:8])

        pP = [P.ps("pP%d" % i, [128, 512]) for i in range(2)]
        cbk = [P.ps("cb%d" % b, [128, 4, 128]) for b in range(3)]
        bA, bB, bC = cbk
        bD, bE = pP
        vA = [bA.h[:, i, :] for i in range(4)]
        vB = [bB.h[:, i, :] for i in range(4)]
        vC = [bC.h[:, i, :] for i in range(4)]
        vD = [bD.h[:, i * 128:(i + 1) * 128] for i in range(4)]
        vE = [bE.h[:, i * 128:(i + 1) * 128] for i in range(4)]
        nt = NormT(P, ident, gn)

        hTg = P.sb("hTg", [128, 16, 512], BF16)
        xb = [P.sb("xb0", [128, D], F32)]
        ub = P.sb("ub", [128, 516], F32)
        halo = P.sb("halo", [128, 8, 4], F32)
        P.memset("dve", halo, halo[:, :, :], 0.0)
        yc = P.sb("yc", [128, 512], F32)
        sc = P.sb("sc", [128, 8, 512], BF16)
        sq = P.sb("sq", [128, 512], BF16)
        rr = P.sb("rr", [128, 512], F32)
        qkTn = P.sb("qkTn", [128, 4, 512], BF16)
        qn32 = P.sb("qn32", [128, 2, 512], F32)
        ktok = P.sb("ktok", [128, 4, 2, 128], BF16)
        vtok = P.sb("vtok", [128, 4, 4, 128], BF16)
        sz = P.sb("sz", [128, 4, 512], F32)
        bag = P.sb("bag", [128, 4, 16], F32)
        S32 = [P.sb("S32_%d" % h, [128, 128], F32) for h in range(4)]
        Sb = S32
        for h in range(4):
            P.memset("dve", S32[h], S32[h][:, :], 0.0)
        g2 = P.sb("g2", [128, 4, 2], F32)
        P.memset("dve", g2, g2[:, :, :], 0.0)
        GM = [P.sb("GM%d" % h, [128, 2, 128], F32) for h in range(4)]
        ex = [P.sb("ex%d" % h, [128, 4], F32) for h in range(4)]
        EE = [P.sb("EE%d" % h, [128, 2, 128], F32) for h in range(4)]
        t1 = [P.sb("t1_%d" % h, [128, 128], F32) for h in range(4)]
        Wb = [P.sb("Wb%d" % h, [128, 128], F32) for h in range(4)]
        WTb = [P.sb("WTb%d" % h, [128, 128], F32) for h in range(4)]
        R32 = [P.sb("R32_%d" % h, [128, 128], F32) for h in range(4)]
        Rb = R32
        qkT = [P.sb("qkT%d" % h, [128, 128], F32) for h in range(4)]
        kbd = [P.sb("kbd%d" % h, [128, 128], F32) for h in range(4)]
        kdec = [P.sb("kdec%d" % h, [128, 128], F32) for h in range(4)]
        vb = [P.sb("vb%d" % h, [128, 128], F32) for h in range(4)]
        u32 = [P.sb("u32_%d" % h, [128, 128], F32) for h in range(4)]
        wTb = [P.sb("wTb%d" % h, [128, 128], F32) for h in range(4)]
        vnew = [P.sb("vnew%d" % h, [128, 128], F32) for h in range(4)]
        o1 = [P.sb("o1_%d" % h, [128, 128], F32) for h in range(4)]
        o32 = [P.sb("o32_%d" % h, [128, 128], F32) for h in range(4)]
        bgs = [P.sb("bgs%d" % h, [128, 4], F32) for h in range(4)]
        oo = [P.sb("oo%d" % i, [128, 512], F32) for i in range(2)]
        junk = nt.junk
        pi = 0
        for g in range(GDN_NG_RUN):
            for j in range(4):
                t = g * 4 + j
                xt = xb[0]
                P.dma("sp", xt, xt[:, :], x_d, x_d[t * 128:(t + 1) * 128, :])
                nt.run(xt, xt[:, :], 128, hTg, lambda c, j=j: hTg[:, c, j * 128:(j + 1) * 128])
            if GDN_DBG < 2:
                continue
            for ch in range(8):
                pp = pP[pi % 2]
                pi += 1
                for c in range(16):
                    P.mm(pp, pp[:, :], wfb, wfb[:, c, ch * 128:(ch + 1) * 128], hTg, hTg[:, c, :],
                         start=(c == 0), stop=(c == 15))
                P.cp("act", ub, ub[:, 0:4], halo, halo[:, ch, :])
                P.cp("act", ub, ub[:, 4:516], pp, pp[:, :])
                P.cp("act", halo, halo[:, ch, :], ub, ub[:, 512:516])
                P.ts("dve", yc, yc[:, :], ub, ub[:, 4:516], cw[:, ch * 4 + 3:ch * 4 + 4], None, ALU.mult,
                     reads=[cw])
                for jj in range(3):
                    P.stt("dve", yc, yc[:, :], ub, ub[:, 1 + jj:513 + jj], cw[:, ch * 4 + jj:ch * 4 + jj + 1],
                          yc, yc[:, :], ALU.mult, ALU.add, reads=[cw])
                P.act(sc, sc[:, ch, :], yc, yc[:, :], AF.Silu)
            if GDN_DBG < 3:
                continue
            for ch in range(4):
                P.act(sq, sq[:, :], sc, sc[:, ch, :], AF.Square)
                pp = pP[pi % 2]
                pi += 1
                P.mm(pp, pp[:, :], onesb, onesb[:, :], sq, sq[:, :])
                if ch < 2:
                    P.ts("dve", rr, rr[:, :], pp, pp[:, :], 128.0, 128.0 * EPS, ALU.mult, ALU.add)
                else:
                    P.ts("dve", rr, rr[:, :], pp, pp[:, :], 1.0, EPS, ALU.mult, ALU.add)
                P.act(rr, rr[:, :], rr, rr[:, :], AF.Ln)
                P.act(rr, rr[:, :], rr, rr[:, :], AF.Exp, scale=-0.5)
                P.tt("dve", qkTn, qkTn[:, ch, :], sc, sc[:, ch, :], rr, rr[:, :], ALU.mult)
                if ch < 2:
                    P.tt("dve", qn32, qn32[:, ch, :], sc, sc[:, ch, :], rr, rr[:, :], ALU.mult)
            if GDN_DBG < 4:
                continue
            for j in range(4):
                tp = nt.tp[j % 2]
                for hh in range(2):
                    P.tr(tp, tp[:, hh, :], qkTn, qkTn[:, 2 + hh, j * 128:(j + 1) * 128], ident, ident[:, :])
                P.cp("act", ktok, ktok[:, j, :, :], tp, tp[:, 0:2, :])
                tp = nt.tp[(j + 1) % 2]
                for hh in range(4):
                    P.tr(tp, tp[:, hh, :], sc, sc[:, 4 + hh, j * 128:(j + 1) * 128], ident, ident[:, :])
                P.cp("dve", vtok, vtok[:, j, :, :], tp, tp[:, :, :])
            if GDN_DBG < 5:
                continue
            for j in range(4):
                pp = pP[pi % 2]
                pi += 1
                for c in range(16):
                    P.mm(pp, pp[:, :], hTg, hTg[:, c, j * 128:(j + 1) * 128], wzb, wzb[:, c, :],
                         start=(c == 0), stop=(c == 15))
                P.act(sz, sz[:, j, :], pp, pp[:, :], AF.Silu)
            for j in range(4):
                pp = pP[pi % 2]
                pi += 1
                for c in range(16):
                    P.mm(pp, pp[:, 0:8], hTg, hTg[:, c, j * 128:(j + 1) * 128], wbab, wbab[:, c, :],
                         start=(c == 0), stop=(c == 15))
                P.act(bag, bag[:, j, 0:4], pp, pp[:, 0:4], AF.Exp, scale=-1.0)
                P.tt("dve", bag, bag[:, j, 12:16], pp, pp[:, 4:8], hp, hp[:, 4:8], ALU.add)
                P.act(bag, bag[:, j, 4:8], bag, bag[:, j, 12:16], AF.Exp)
                P.ts("dve", bag, bag[:, j, 0:8], bag, bag[:, j, 0:8], 1.0, None, ALU.add)
                P.op("dve", lambda e, j=j: e.reciprocal(bag[:, j, 0:4], bag[:, j, 0:4]), [bag], [bag])
                P.act(bag, bag[:, j, 4:8], bag, bag[:, j, 4:8], AF.Ln)
                P.tt("dve", bag, bag[:, j, 4:8], bag, bag[:, j, 4:8], nA, nA[:, :], ALU.mult)
                P.ts("dve", bag, bag[:, j, 8:12], bag, bag[:, j, 0:4], -1.0, None, ALU.mult)
            if GDN_DBG < 6:
                continue
            for j in range(4):
                t = g * 4 + j
                tsl = slice(j * 128, (j + 1) * 128)
                for h in range(4):
                    qh = h // 2
                    gcol = bag[:, j, 4 + h:5 + h]
                    beta = bag[:, j, h:h + 1]
                    nbeta = bag[:, j, 8 + h:9 + h]
                    P.ts("dve", GM[h], GM[h][:, 0, :], mk, Mle, gcol, None, ALU.mult, reads=[bag])
                    P.ts("dve", GM[h], GM[h][:, 1, :], mk, Mgt, gcol, None, ALU.mult, reads=[bag])
                    P.cp("act", g2, g2[0:64, h, 0:1], bag, bag[0:64, j, 4 + h:5 + h])
                    P.cp("act", g2, g2[64:128, h, 1:2], bag, bag[64:128, j, 4 + h:5 + h])
                    sm = vA[0]
                    P.mm(bA, sm[:, 0:1], mk, Mle, bag, gcol)
                    P.mm(bA, sm[:, 1:2], mk, Mgt, bag, gcol)
                    P.mm(bA, sm[:, 2:4], ones32, ones32[:, :], g2, g2[:, h, :])
                    P.act(ex[h], ex[h][:, :], bA, sm[:, 0:4], AF.Exp)
                    if GDN_DBG < 7:
                        continue
                    P.mm(bA, vA[1], GM[h], GM[h][:, 0, :], mk, Mgt)
                    P.mm(bA, vA[2], GM[h], GM[h][:, 1, :], mk, Mle)
                    P.act(EE[h], EE[h][:, 0, :], bA, vA[1], AF.Exp)
                    P.act(EE[h], EE[h][:, 1, :], bA, vA[2], AF.Exp)
                    P.mm(bB, vB[0], qkTn, qkTn[:, 2 + qh, tsl], qkTn, qkTn[:, 2 + qh, tsl])
                    P.mm(bB, vB[1], qkTn, qkTn[:, 2 + qh, tsl], qkTn, qkTn[:, qh, tsl])
                    P.tt("dve", t1[h], t1[h][:, :], bB, vB[0], EE[h], EE[h][:, 0, :], ALU.mult)
                    P.stt("dve", WTb[h], WTb[h][:, :], t1[h], t1[h][:, :], nbeta, mk, Mgt,
                          ALU.mult, ALU.mult, reads=[bag])
                    P.tt("dve", t1[h], t1[h][:, :], bB, vB[1], EE[h], EE[h][:, 1, :], ALU.mult)
                    P.tt("dve", qkT[h], qkT[h][:, :], t1[h], t1[h][:, :], mk, Mle, ALU.mult)
                    if GDN_DBG < 8:
                        continue
                    P.tr(bB, vB[3], WTb[h], WTb[h][:, :], idf, idf[:, :])
                    P.cp("dve", Wb[h], Wb[h][:, :], bB, vB[3])
                    P.tt("dve", R32[h], R32[h][:, :], Wb[h], Wb[h][:, :], idf, idf[:, :], ALU.add)
                    if GDN_DBG < 9:
                        continue
                    for k in range(5):
                        if k < 4:
                            P.mm(bB, vB[0], WTb[h], WTb[h][:, :], Wb[h], Wb[h][:, :])
                        P.mm(bA, vA[3], Wb[h], Wb[h][:, :], WTb[h], WTb[h][:, :])
                        if k < 4:
                            P.cp("dve", Wb[h], Wb[h][:, :], bB, vB[0])
                        P.cp("act", WTb[h], WTb[h][:, :], bA, vA[3])
                        P.mm(bB, vB[2], WTb[h], WTb[h][:, :], Rb[h], Rb[h][:, :])
                        P.tt("dve", R32[h], R32[h][:, :], R32[h], R32[h][:, :], bB, vB[2], ALU.add)
                    if GDN_DBG < 10:
                        continue
                    P.tt("dve", bgs[h], bgs[h][:, 0:1], bag, beta, ex[h], ex[h][:, 0:1], ALU.mult)
                    P.ts("dve", kbd[h], kbd[h][:, :], ktok, ktok[:, j, qh, :], bgs[h][:, 0:1], None, ALU.mult,
                         reads=[bgs[h]])
                    P.ts("dve", kdec[h], kdec[h][:, :], ktok, ktok[:, j, qh, :], ex[h][:, 1:2], None, ALU.mult,
                         reads=[ex[h]])
                    P.ts("dve", vb[h], vb[h][:, :], vtok, vtok[:, j, h, :], beta, None, ALU.mult, reads=[bag])
                    P.mm(bA, vA[1], Rb[h], Rb[h][:, :], vb[h], vb[h][:, :])
                    P.mm(bA, vA[2], kbd[h], kbd[h][:, :], Rb[h], Rb[h][:, :])
                    P.cp("act", u32[h], u32[h][:, :], bA, vA[1])
                    P.cp("act", wTb[h], wTb[h][:, :], bA, vA[2])
                if GDN_DBG < 11:
                    continue
                for c in range(2):
                    ps_ = slice(c * 64, (c + 1) * 64)
                    for h in range(4):
                        P.mm(bC, vC[h], wTb[h], wTb[h][:, :], Sb[h], Sb[h][:, :])
                    for h in range(4):
                        P.tt("dve", vnew[h], vnew[h][ps_, :], u32[h], u32[h][ps_, :], bC, vC[h][ps_, :], ALU.subtract)
                    for h in range(4):
                        qh = h // 2
                        P.mm(bD, vD[h], qn32, qn32[:, qh, tsl], Sb[h], Sb[h][:, :])
                    for h in range(4):
                        P.mm(bE, vE[h], qkT[h], qkT[h][ps_, :], vnew[h], vnew[h][ps_, :])
                    for h in range(4):
                        P.mm(bC, vC[h], kdec[h], kdec[h][ps_, :], vnew[h], vnew[h][ps_, :])
                    for h in range(4):
                        P.op("act", lambda e, h=h, ps_=ps_: e.activation(
                            o1[h][ps_, :], vD[h][ps_, :], AF.Copy, scale=ex[h][ps_, 0:1]), [bD, ex[h]], [o1[h]])
                    for h in range(4):
                        P.stt("dve", S32[h], S32[h][:, :], S32[h], S32[h][:, :], ex[h][:, 2 + c:3 + c],
                              bC, vC[h], ALU.mult, ALU.add, reads=[ex[h]])
                    for h in range(4):
                        P.tt("dve", o32[h], o32[h][ps_, :], o1[h], o1[h][ps_, :], bE, vE[h][ps_, :], ALU.add)
                if GDN_DBG < 12:
                    continue
                ot = oo[t % 2]
                for h in range(4):
                    P.memset("dve", bgs[h], bgs[h][:, 1:2], 0.0)
                    P.op("act", lambda e, h=h: e.activation(junk[:, 0:128], o32[h][:, :], AF.Square,
                                                            accum_out=bgs[h][:, 1:2]), [o32[h], bgs[h]],
                         [junk, bgs[h]])
                    P.ts("dve", bgs[h], bgs[h][:, 2:3], bgs[h], bgs[h][:, 1:2], 1.0 / 128.0, EPS, ALU.mult, ALU.add)
                    P.act(bgs[h], bgs[h][:, 2:3], bgs[h], bgs[h][:, 2:3], AF.Ln)
                    P.act(bgs[h], bgs[h][:, 2:3], bgs[h], bgs[h][:, 2:3], AF.Exp, scale=-0.5)
                    P.stt("dve", o1[h], o1[h][:, :], o32[h], o32[h][:, :], bgs[h][:, 2:3], gno, gno[:, :],
                          ALU.mult, ALU.mult, reads=[bgs[h]])
                    P.tt("dve", ot, ot[:, h * 128:(h + 1) * 128], o1[h], o1[h][:, :], sz, sz[:, j, h * 128:(h + 1) * 128],
                         ALU.mult)
                P.dma("sp", o_d, o_d[t * 128:(t + 1) * 128, :], ot, ot[:, :], semkey="st%d" % (t % 2))
        P.wait_all("sp", [o_d])
        P.replay()
    return nc


def gdn_masks():
    t = np.arange(128)[:, None]
    i = np.arange(128)[None, :]
    same = (t // 64) == (i // 64)
    mle = (same & (t <= i)).astype(np.float32)
    mgt = (same & (t > i)).astype(np.float32)
    return np.ascontiguousarray(np.stack([mle, mgt], axis=1))


def run_gdn(x, inp):
    nc = build_gdn()
    ident = np.eye(128, dtype=np.float32)
    gn = _gn_layout(inp["mixer_norm"][1])
    w_in = inp["gdn_w_in"][0]
    cwf = inp["gdn_conv_w"][0]
    masks = gdn_masks()
    gno = np.ascontiguousarray(np.broadcast_to(inp["gdn_norm"][0][None], (128, 128))).astype(np.float32)
    in_maps = []
    for c in range(NCORES):
        qs = slice(c * 256, (c + 1) * 256)
        ks = slice(2048 + c * 256, 2048 + (c + 1) * 256)
        vs = slice(4096 + c * 512, 4096 + (c + 1) * 512)
        zs = slice(8192 + c * 512, 8192 + (c + 1) * 512)
        wf = np.ascontiguousarray(np.concatenate([w_in[:, qs], w_in[:, ks], w_in[:, vs]], axis=1))
        wz = np.ascontiguousarray(w_in[:, zs])
        wba = np.ascontiguousarray(np.concatenate([w_in[:, 12288 + c * 4:12288 + c * 4 + 4],
                                                   w_in[:, 12320 + c * 4:12320 + c * 4 + 4]], axis=1))
        cwc = np.concatenate([cwf[:, qs], cwf[:, ks], cwf[:, vs]], axis=1)
        cw = np.ascontiguousarray(cwc.reshape(4, 8, 128).transpose(2, 1, 0).reshape(128, 32))
        hp = np.concatenate([inp["gdn_A_log"][0][c * 4:c * 4 + 4], inp["gdn_dt_bias"][0][c * 4:c * 4 + 4]])
        hp = np.ascontiguousarray(np.broadcast_to(hp[None], (128, 8))).astype(np.float32)
        in_maps.append({"x": x, "ident": ident, "gn": gn, "wf": wf, "wz": wz, "wba": wba, "cw": cw, "hp": hp,
                        "gno": gno, "masks": masks})
    res = run_bass_kernel_spmd(nc, in_maps, core_ids=list(range(NCORES)), trace=ATT_TRACE)
    if ATT_TRACE:
        print("GDN exec_time_ns", res.exec_time_ns)
    return np.concatenate([res.results[c]["o"] for c in range(NCORES)], axis=1)


def kernel(x, mixer_norm, ffn_norm, diff_w_qkv, diff_q_norm, diff_k_norm, diff_lambda_q1,
           diff_lambda_k1, diff_lambda_q2, diff_lambda_k2, diff_subln, diff_w_o, gdn_w_in,
           gdn_conv_w, gdn_A_log, gdn_dt_bias, gdn_norm, gdn_w_out, ffn_w_up, ffn_conv_w,
           ffn_conv_b, ffn_w_down):
    f = lambda a: np.ascontiguousarray(np.asarray(a, dtype=np.float32))
    inp = {"mixer_norm": f(mixer_norm), "diff_w_qkv": f(diff_w_qkv), "diff_q_norm": f(diff_q_norm),
           "diff_k_norm": f(diff_k_norm), "diff_lambda_q1": f(diff_lambda_q1),
           "diff_lambda_k1": f(diff_lambda_k1), "diff_lambda_q2": f(diff_lambda_q2),
           "diff_lambda_k2": f(diff_lambda_k2), "diff_subln": f(diff_subln), "gdn_w_in": f(gdn_w_in),
           "gdn_conv_w": f(gdn_conv_w), "gdn_A_log": f(gdn_A_log), "gdn_dt_bias": f(gdn_dt_bias),
           "gdn_norm": f(gdn_norm)}
    x0 = f(x)[0]
    ffn_norm, ffn_w_up, ffn_conv_w = f(ffn_norm), f(ffn_w_up), f(ffn_conv_w)
    ffn_conv_b, ffn_w_down = f(ffn_conv_b), f(ffn_w_down)
    o = run_att(x0, inp)
    x1 = run_oproj(o, x0, f(diff_w_o)[0])
    x2 = run_ffn(x1, ffn_norm[0], ffn_w_up[0], ffn_conv_w[0], ffn_conv_b[0], ffn_w_down[0])
    o2 = run_gdn(x2, inp)
    x3 = run_oproj(o2, x2, f(gdn_w_out)[0])
    x4 = run_ffn(x3, ffn_norm[1], ffn_w_up[1], ffn_conv_w[1], ffn_conv_b[1], ffn_w_down[1])
    return x4[None].astype(np.float32)
```
